# Optimizing a Trainium2 kernel written in Bass

```python
import jax, jax.numpy as jnp
from jax import lax
import numpy as np

D_MODEL = 4096
BATCH = 2
SEQ = 8192
DEPTH = 1

HEAD_DIM = 128
DILATED_GROUPS = ((128, 1), (512, 4), (2048, 16))
N_GROUPS = len(DILATED_GROUPS)
HEADS_PER_GROUP = D_MODEL // 512
N_HEADS = HEADS_PER_GROUP * N_GROUPS
QKV_WIDTH = N_HEADS * HEAD_DIM
ATTN_WIDTH = HEADS_PER_GROUP * HEAD_DIM
FOURIER_WIDTH = D_MODEL // 2
FOURIER_GROUPS = 8
FOURIER_GROUP_DIM = FOURIER_WIDTH // FOURIER_GROUPS
N_BRANCHES = 2
ROPE_THETA = 10000.0
NORM_EPS = 1e-6
NEG_INF = -1e30
IN_SPLITS = (QKV_WIDTH, QKV_WIDTH, QKV_WIDTH, ATTN_WIDTH, FOURIER_WIDTH, FOURIER_WIDTH, N_BRANCHES * D_MODEL)
IN_WIDTH = sum(IN_SPLITS)

kernel_name = "hybrid_dilated_attn_fnet_gated_block"


def rms_norm(x, gain):
    xf = x.astype(jnp.float32)
    y = xf * lax.rsqrt(jnp.mean(xf * xf, axis=-1, keepdims=True) + NORM_EPS)
    return (y * gain.astype(jnp.float32)).astype(x.dtype)


def rotary(t, positions):
    half = t.shape[-1] // 2
    inv_freq = ROPE_THETA ** (-jnp.arange(half, dtype=jnp.float32) * (2.0 / t.shape[-1]))
    ang = positions.astype(jnp.float32)[:, None] * inv_freq[None, :]
    cos = jnp.cos(ang)[None, :, None, :]
    sin = jnp.sin(ang)[None, :, None, :]
    t1, t2 = t[..., :half], t[..., half:]
    return jnp.concatenate([t1 * cos - t2 * sin, t2 * cos + t1 * sin], axis=-1)


def banded_attention(q, k, v, radius):
    n, L, h, dh = q.shape
    blk = radius
    nb = -(-L // blk)
    pad = nb * blk - L
    q = jnp.pad(q, ((0, 0), (0, pad), (0, 0), (0, 0))).reshape(n, nb, blk, h, dh)
    kv_pad = ((0, 0), (blk, pad + blk), (0, 0), (0, 0))
    k = jnp.pad(k, kv_pad).reshape(n, nb + 2, blk, h, dh)
    v = jnp.pad(v, kv_pad).reshape(n, nb + 2, blk, h, dh)

    def neighbours(t):
        return jnp.concatenate([t[:, :-2], t[:, 1:-1], t[:, 2:]], axis=2)

    kb, vb = neighbours(k), neighbours(v)
    s = jnp.einsum('nbqhd,nbkhd->nbhqk', q, kb) * (dh ** -0.5)
    qpos = jnp.arange(nb)[:, None] * blk + jnp.arange(blk)[None, :]
    kpos = (jnp.arange(nb)[:, None] - 1) * blk + jnp.arange(3 * blk)[None, :]
    rel = kpos[:, None, :] - qpos[:, :, None]
    valid = (jnp.abs(rel) <= radius) & (kpos[:, None, :] >= 0) & (kpos[:, None, :] < L)
    s = jnp.where(valid[None, :, None], s, NEG_INF)
    lse = jax.nn.logsumexp(s, axis=-1)
    p = jnp.exp(s - lse[..., None])
    o = jnp.einsum('nbhqk,nbkhd->nbqhd', p, vb).reshape(n, nb * blk, h, dh)[:, :L]
    lse = lse.transpose(0, 1, 3, 2).reshape(n, nb * blk, h)[:, :L]
    return o, lse


def dilated_attention(q, k, v, window, dilation):
    b, s, h, dh = q.shape
    sub = s // dilation

    def to_classes(t):
        return t.reshape(b, sub, dilation, h, dh).transpose(0, 2, 1, 3, 4).reshape(b * dilation, sub, h, dh)

    o, lse = banded_attention(to_classes(q), to_classes(k), to_classes(v), window // (2 * dilation))
    o = o.reshape(b, dilation, sub, h, dh).transpose(0, 2, 1, 3, 4).reshape(b, s, h, dh)
    lse = lse.reshape(b, dilation, sub, h).transpose(0, 2, 1, 3).reshape(b, s, h)
    return o, lse


def setup_inputs(seed: int = 0) -> dict:
    key = jax.random.key(seed)
    ks = jax.random.split(key, 9)
    f32 = jnp.float32
    x = jax.random.normal(ks[0], (BATCH, SEQ, D_MODEL), f32)
    norm_gain = 1.0 + 0.02 * jax.random.normal(ks[1], (DEPTH, D_MODEL), f32)
    w_in = jax.random.normal(ks[2], (DEPTH, D_MODEL, IN_WIDTH), f32) * D_MODEL ** -0.5
    gate_bias = 0.01 * jax.random.normal(ks[3], (DEPTH, N_BRANCHES * D_MODEL), f32)
    w_branch_attn = jax.random.normal(ks[4], (DEPTH, ATTN_WIDTH, D_MODEL), f32) * ATTN_WIDTH ** -0.5
    w_branch_fourier = jax.random.normal(ks[5], (DEPTH, FOURIER_WIDTH, D_MODEL), f32) * FOURIER_WIDTH ** -0.5
    w_out = jax.random.normal(ks[6], (DEPTH, D_MODEL, D_MODEL), f32) * D_MODEL ** -0.5
    final_norm_gain = 1.0 + 0.02 * jax.random.normal(ks[7], (D_MODEL,), f32)
    return {"x": x, "norm_gain": norm_gain, "w_in": w_in, "gate_bias": gate_bias,
            "w_branch_attn": w_branch_attn, "w_branch_fourier": w_branch_fourier,
            "w_out": w_out, "final_norm_gain": final_norm_gain}


def reference(x, norm_gain, w_in, gate_bias, w_branch_attn, w_branch_fourier, w_out, final_norm_gain):
    b, s, _ = x.shape
    dtype = x.dtype
    positions = jnp.arange(s, dtype=jnp.int32)
    offsets = np.cumsum(IN_SPLITS)[:-1].tolist()
    for layer in range(DEPTH):
        h = rms_norm(x, norm_gain[layer])
        proj = h @ w_in[layer]
        q, k, v, z_attn, u_four, z_four, gates = jnp.split(proj, offsets, axis=-1)

        q = rotary(q.astype(jnp.float32).reshape(b, s, N_HEADS, HEAD_DIM), positions)
        k = rotary(k.astype(jnp.float32).reshape(b, s, N_HEADS, HEAD_DIM), positions)
        v = v.astype(jnp.float32).reshape(b, s, N_HEADS, HEAD_DIM)
        outs, lses = [], []
        for g, (window, dilation) in enumerate(DILATED_GROUPS):
            hs = slice(g * HEADS_PER_GROUP, (g + 1) * HEADS_PER_GROUP)
            o_g, lse_g = dilated_attention(q[:, :, hs], k[:, :, hs], v[:, :, hs], window, dilation)
            outs.append(o_g)
            lses.append(lse_g)
        outs = jnp.stack(outs, axis=0)
        alpha = jax.nn.softmax(jnp.stack(lses, axis=0), axis=0)
        o_attn = jnp.sum(alpha[..., None] * outs, axis=0).reshape(b, s, ATTN_WIDTH).astype(dtype)
        branch_attn = (o_attn * jax.nn.silu(z_attn)) @ w_branch_attn[layer]

        u = u_four.astype(jnp.float32).reshape(b, s, FOURIER_GROUPS, FOURIER_GROUP_DIM)
        y_four = jnp.fft.fft2(u, axes=(1, 3), norm="ortho").real.reshape(b, s, FOURIER_WIDTH).astype(dtype)
        branch_four = (y_four * jax.nn.silu(z_four)) @ w_branch_fourier[layer]

        gate_logits = gates + gate_bias[layer]
        g_attn, g_four = jnp.split(gate_logits, N_BRANCHES, axis=-1)
        mixed = jax.nn.sigmoid(g_attn) * branch_attn + jax.nn.sigmoid(g_four) * branch_four
        x = x + mixed @ w_out[layer]
    return rms_norm(x, final_norm_gain)
```

```python
import os
import numpy as np
import ml_dtypes
import concourse.bass as bass
import concourse.mybir as mybir
from concourse.bass_utils import run_bass_kernel_spmd

F32 = mybir.dt.float32
BF16 = mybir.dt.bfloat16
AF = mybir.ActivationFunctionType
ALU = mybir.AluOpType
AX = mybir.AxisListType

D = 4096
SEQ = 8192
OWN = 2048
EXT = 4096
HALO = 1024
QKV = 3072
AW = 1024
FW = 2048
INW = 22528
OFF_Q, OFF_K, OFF_V, OFF_ZA, OFF_U, OFF_ZF, OFF_G = 0, 3072, 6144, 9216, 10240, 12288, 14336
KC = D // 128
TT = 1024
EPS = 1e-6


class T:
    __slots__ = ("name", "writers", "readers", "sem", "dcount")

    def __init__(self, name):
        self.name = name
        self.writers = {}
        self.readers = {}
        self.sem = None
        self.dcount = 0


class Op:
    __slots__ = ("eng", "fn", "deps", "sig", "sigcount", "dma_t", "dma_cnt")


class Prog:
    ENGS = ("pe", "act", "dve", "pool", "sp")

    def __init__(self, nc):
        self.nc = nc
        self.ops = []
        self.last = {}
        self.dma_tiles = {}

    def op(self, eng, fn, reads=(), writes=(), dma=None):
        o = Op()
        o.eng, o.fn, o.sig, o.sigcount, o.dma_t, o.dma_cnt = eng, fn, False, 0, dma, 0
        idx = len(self.ops)
        deps = {}

        def add(d):
            for k, v in d.items():
                if deps.get(k, -1) < v:
                    deps[k] = v

        for t in reads:
            add(t.writers)
        rset = set(id(t) for t in reads)
        for t in writes:
            if t.readers or id(t) in rset:
                add(t.readers)
                add(t.writers)
        o.deps = deps
        if dma is not None:
            dma.dcount += 16
            o.dma_cnt = dma.dcount
            ev = (("d", dma), dma.dcount)
        else:
            ev = (("c", eng), idx)
        newgen = [t for t in writes if (t.readers or id(t) in rset)]
        for t in reads:
            if t.readers.get(ev[0], -1) < ev[1]:
                t.readers[ev[0]] = ev[1]
        for t in writes:
            if any(t is g for g in newgen):
                t.writers = {ev[0]: ev[1]}
                t.readers = {}
            else:
                if t.writers.get(ev[0], -1) < ev[1]:
                    t.writers[ev[0]] = ev[1]
        self.ops.append(o)
        if dma is None and fn is not None:
            self.last[eng] = idx
        if dma is not None:
            self.dma_tiles[id(dma)] = dma
        return o

    def barrier(self):
        deps = {}
        for e, idx in self.last.items():
            deps[("c", e)] = idx
        for t in self.dma_tiles.values():
            if t.dcount:
                deps[("d", t)] = t.dcount
        for e in self.ENGS:
            o = Op()
            o.eng, o.fn, o.sig, o.sigcount, o.dma_t, o.dma_cnt = e, None, False, 0, None, 0
            o.deps = dict(deps)
            self.ops.append(o)

    def emit(self):
        nc = self.nc
        ops = self.ops
        for o in ops:
            for k, v in o.deps.items():
                if k[0] == "c":
                    if k[1] == "pe" and o.eng == "pe":
                        continue
                    ops[v].sig = True
        cnt = {e: 0 for e in self.ENGS}
        for o in ops:
            if o.sig:
                cnt[o.eng] += 1
                o.sigcount = cnt[o.eng]
        esem = {e: nc.alloc_semaphore("sem_" + e) for e in self.ENGS}
        waited = {e: {} for e in self.ENGS}
        per = {e: [] for e in self.ENGS}
        for o in ops:
            ws = []
            for k, v in o.deps.items():
                if k[0] == "c":
                    if k[1] == "pe" and o.eng == "pe":
                        continue
                    val = ops[v].sigcount
                    sem = esem[k[1]]
                    key = k
                else:
                    t = k[1]
                    if t.sem is None:
                        t.sem = nc.alloc_semaphore("dsem_" + t.name)
                    val = v
                    sem = t.sem
                    key = ("d", id(t))
                if waited[o.eng].get(key, 0) >= val:
                    continue
                waited[o.eng][key] = val
                ws.append((sem, val))
            per[o.eng].append((o, ws))
        for o in ops:
            if o.dma_t is not None and o.dma_t.sem is None:
                o.dma_t.sem = nc.alloc_semaphore("dsem_" + o.dma_t.name)

        def run(eng_name, eng):
            for o, ws in per[eng_name]:
                for sem, val in ws:
                    eng.wait_ge(sem, val)
                if o.fn is None:
                    continue
                ins = o.fn(eng)
                if o.dma_t is not None:
                    ins.then_inc(o.dma_t.sem, 16)
                elif o.sig:
                    ins.then_inc(esem[eng_name], 1)

        with nc.Block() as block:
            @block.tensor
            def _(e):
                run("pe", e)

            @block.scalar
            def _(e):
                run("act", e)

            @block.vector
            def _(e):
                run("dve", e)

            @block.gpsimd
            def _(e):
                run("pool", e)

            @block.sync
            def _(e):
                run("sp", e)


def bc_ap(ap2d, n_mid, inner):
    a = ap2d.ap
    return bass.AP(ap2d.tensor, ap2d.offset, [list(a[0]), list(a[1]), [0, inner]])


def bc_mid(ap2d, n_mid):
    a = ap2d.ap
    return bass.AP(ap2d.tensor, ap2d.offset, [list(a[0]), [0, n_mid], list(a[1])])


def build(cfg):
    nc = bass.Bass("TRN2", target_bir_lowering=False)
    P = Prog(nc)
    dbg = cfg.get("dbg", False)
    okind = "ExternalOutput" if dbg else "Internal"

    def din(name, shape, dt=F32):
        return nc.dram_tensor(name, list(shape), dt, kind="ExternalInput").ap()

    x_b = din("x_b", [SEQ, D])
    x_e = din("x_e", [EXT, D])
    w_in = din("w_in", [D, INW])
    w_ba = din("w_ba", [AW, D])
    w_bf = din("w_bf", [FW, D])
    w_out = din("w_out", [D, D])
    gainT_d = din("gainT", [128, KC])
    gbiasT_d = din("gbiasT", [128, 64])
    rope_d = din("rope", [128, 2, EXT // 128, 64])
    fgain_d = din("fgain", [1, D])
    mask_d = din("maskAB", [128, 2, 128], BF16)
    kval_d = din("kval", [128, 69])
    ma_d = din("MA", [128, 64, 256], BF16)
    dc_d = din("DC", [128, 2, 2, 512], BF16)
    cs_d = din("CS", [128, 2, 32], BF16)
    out_d = nc.dram_tensor("out", [OWN, D], F32, kind="ExternalOutput").ap()
    S_A = nc.dram_tensor("S_A", [AW, OWN], BF16, kind=okind).ap()
    S_F = nc.dram_tensor("S_F", [FW, OWN], BF16, kind=okind).ap()
    d_scr2 = T("d_scr2")
    WB_ba = nc.dram_tensor("WB_ba", [AW, D], BF16).ap()
    WB_bf = nc.dram_tensor("WB_bf", [FW, D], BF16).ap()
    WB_out = nc.dram_tensor("WB_out", [D, D], BF16).ap()
    d_wb = T("d_wb")
    t_wcast = T("wcast")

    S_q = nc.dram_tensor("S_q", [OWN, QKV], BF16, kind=okind).ap()
    S_k = nc.dram_tensor("S_k", [EXT, QKV], BF16, kind=okind).ap()
    S_v = nc.dram_tensor("S_v", [EXT, QKV], BF16, kind=okind).ap()
    S_u = nc.dram_tensor("S_u", [SEQ, FW], BF16, kind=okind).ap()
    S_za = nc.dram_tensor("S_za", [AW, OWN], BF16, kind=okind).ap()
    S_zf = nc.dram_tensor("S_zf", [FW, OWN], BF16, kind=okind).ap()
    S_g = nc.dram_tensor("S_g", [2 * D, OWN], BF16, kind=okind).ap()
    d_scr = T("d_scr")

    sbp = {"off": 16512, "n": 0}

    def sb(name, shape, dt):
        nbytes = int(np.prod(shape[1:])) * (4 if dt == F32 else 2)
        nbytes = (nbytes + 63) // 64 * 64
        off = sbp["off"]
        sbp["off"] += nbytes
        assert sbp["off"] <= 229376, (name, sbp["off"])
        sbp["n"] += 1
        return nc.alloc_sbuf_tensor_at("%s_%d" % (name, sbp["n"]), list(shape), dt, offset=off)

    ident = sb("ident", [128, 128], BF16)
    identf = sb("identf", [128, 128], F32)
    t_ident = T("ident")
    gainT = sb("gainT_s", [128, KC], F32)
    t_gainT = T("gainT")
    gbiasT = sb("gbiasT_s", [128, 64], F32)
    t_gbiasT = T("gbiasT")
    rope = sb("rope_s", [128, 2, EXT // 128, 64], F32)
    t_rope = T("rope")
    P.op("sp", lambda e: e.dma_start(out=gainT[:, :], in_=gainT_d[:, :]), writes=[t_gainT], dma=t_gainT)
    P.op("sp", lambda e: e.dma_start(out=gbiasT[:, :], in_=gbiasT_d[:, :]), writes=[t_gbiasT], dma=t_gbiasT)
    P.op("sp", lambda e: e.dma_start(out=rope[:, :, :, :], in_=rope_d[:, :, :, :]), writes=[t_rope], dma=t_rope)
    P.op("pool", lambda e: e.memset(identf[:, :], 1.0), writes=[t_ident])
    P.op("pool", lambda e: e.affine_select(out=identf[:, :], in_=identf[:, :], pattern=[[-1, 128]],
                                           compare_op=ALU.is_equal, fill=0.0, base=0, channel_multiplier=1),
         reads=[t_ident], writes=[t_ident])
    P.op("pool", lambda e: e.tensor_copy(out=ident[:, :], in_=identf[:, :]), reads=[t_ident], writes=[t_ident])
    epsb = sb("epsb", [128, 1], F32)
    P.op("pool", lambda e: e.memset(epsb[:, :], EPS), writes=[t_ident])

    sb_mark = sbp["off"]
    hT = sb("hT", [128, KC, TT], BF16)
    t_hT = T("hT")
    xbuf = [sb("xbuf%d" % i, [128, D], F32) for i in range(2)]
    t_xbuf = [T("xbuf%d" % i) for i in range(2)]
    xn = [sb("xn%d" % i, [128, D], BF16) for i in range(2)]
    t_xn = [T("xn%d" % i) for i in range(2)]
    ss = sb("ss", [128, 8], F32)
    t_ss = T("ss")
    NW = 2
    wslot = [sb("wslot%d" % i, [128, KC, 512], BF16) for i in range(NW)]
    t_wslot = [T("wslot%d" % i) for i in range(NW)]
    NST = 4
    stage = [sb("stage%d" % i, [128, 512], BF16) for i in range(NST)]
    t_stage = [T("stage%d" % i) for i in range(NST)]
    tmpA = sb("tmpA", [128, 256], F32)
    tmpB = sb("tmpB", [128, 256], F32)
    t_tmpA, t_tmpB = T("tmpA"), T("tmpB")
    NPS = 6
    ps = [nc.alloc_psum_tensor("ps%d" % i, [128, 512], F32) for i in range(NPS)]
    t_ps = [T("ps%d" % i) for i in range(NPS)]
    pst = [nc.alloc_psum_tensor("pst%d" % i, [128, 1024], BF16) for i in range(2)]
    t_pst = [T("pst%d" % i) for i in range(2)]
    ctr = {"ps": 0, "st": 0, "w": 0, "x": 0, "pst": 0}

    def norm_tile(xsrc, row0, nsub):
        for s in range(nsub):
            i = ctr["x"] % 2
            ctr["x"] += 1
            r0 = row0 + s * 128
            P.op("sp", lambda e, i=i, r0=r0: e.dma_start(out=xbuf[i][:, :], in_=xsrc[r0:r0 + 128, :]),
                 writes=[t_xbuf[i]], dma=t_xbuf[i])
            P.op("act", lambda e, i=i: e.activation(out=xn[i][:, :], in_=xbuf[i][:, :], func=AF.Square,
                                                    accum_out=ss[:, 0:1]),
                 reads=[t_xbuf[i]], writes=[t_xn[i], t_ss])
            P.op("act", lambda e: e.activation(out=ss[:, 1:2], in_=ss[:, 0:1], func=AF.Sqrt, bias=epsb[:, 0:1],
                                               scale=1.0 / D), reads=[t_ss, t_ident], writes=[t_ss])
            P.op("dve", lambda e: e.reciprocal(out=ss[:, 2:3], in_=ss[:, 1:2]), reads=[t_ss], writes=[t_ss])
            P.op("dve", lambda e, i=i: e.tensor_scalar(out=xn[i][:, :], in0=xbuf[i][:, :], scalar1=ss[:, 2:3],
                                                       scalar2=None, op0=ALU.mult),
                 reads=[t_xbuf[i], t_ss], writes=[t_xn[i]])
            for g in range(4):
                b = ctr["pst"] % 2
                ctr["pst"] += 1
                for kk in range(8):
                    k = g * 8 + kk
                    P.op("pe", lambda e, b=b, kk=kk, k=k, i=i: e.transpose(pst[b][:, kk * 128:(kk + 1) * 128],
                                                                        xn[i][:, k * 128:(k + 1) * 128], ident[:, :]),
                         reads=[t_xn[i], t_ident], writes=[t_pst[b]])
                P.op("dve", lambda e, b=b, g=g, s=s: e.tensor_tensor(
                    out=hT[:, g * 8:(g + 1) * 8, s * 128:(s + 1) * 128],
                    in0=pst[b][:, :].rearrange("p (a b) -> p a b", b=128),
                    in1=bc_ap(gainT[:, g * 8:(g + 1) * 8], 8, 128), op=ALU.mult),
                     reads=[t_pst[b], t_gainT], writes=[t_hT])

    def load_w(wsrc, c0):
        i = ctr["w"] % NW
        ctr["w"] += 1
        src = wsrc[:, c0:c0 + 512].rearrange("(k p) c -> p k c", p=128)
        P.op("pool", lambda e, i=i, src=src: e.dma_start(out=wslot[i][:, :, :], in_=src),
             writes=[t_wslot[i]], dma=t_wslot[i])
        return i

    def next_ps():
        i = ctr["ps"] % NPS
        ctr["ps"] += 1
        return i

    def next_stage():
        i = ctr["st"] % NST
        ctr["st"] += 1
        return i

    def gemm_tm(wi, nsub, epi):
        for s in (range(nsub) if isinstance(nsub, int) else nsub):
            pi = next_ps()
            for k in range(KC):
                P.op("pe", lambda e, pi=pi, k=k, s=s: e.matmul(ps[pi][:, :], lhsT=hT[:, k, s * 128:(s + 1) * 128],
                                                            rhs=wslot[wi][:, k, :], start=(k == 0), stop=(k == KC - 1)),
                     reads=[t_hT, t_wslot[wi]], writes=[t_ps[pi]])
            epi(pi, s)

    def gemm_fm(wi, nhalf, epi):
        for cb in range(4):
            for h in range(nhalf):
                pi = next_ps()
                for k in range(KC):
                    P.op("pe", lambda e, pi=pi, k=k, cb=cb, h=h: e.matmul(
                        ps[pi][:, :], lhsT=wslot[wi][:, k, cb * 128:(cb + 1) * 128],
                        rhs=hT[:, k, h * 512:(h + 1) * 512], start=(k == 0), stop=(k == KC - 1)),
                         reads=[t_hT, t_wslot[wi]], writes=[t_ps[pi]])
                epi(pi, cb, h)

    def epi_copy(dst, tok0, col0, eng):
        def f(pi, s):
            si = next_stage()
            if eng == "act":
                P.op("act", lambda e: e.activation(out=stage[si][:, :], in_=ps[pi][:, :], func=AF.Copy),
                     reads=[t_ps[pi]], writes=[t_stage[si]])
            else:
                P.op("dve", lambda e: e.tensor_copy(out=stage[si][:, :], in_=ps[pi][:, :]),
                     reads=[t_ps[pi]], writes=[t_stage[si]])
            r0 = tok0 + s * 128
            P.op("sp", lambda e: e.dma_start(out=dst[r0:r0 + 128, col0:col0 + 512], in_=stage[si][:, :]),
                 reads=[t_stage[si]], writes=[d_scr], dma=t_stage[si])
        return f

    def epi_rope(dst, tok0, col0, esub0):
        def f(pi, s):
            si = next_stage()
            es = esub0 + s
            cosb = bc_mid(rope[:, 0, es, :], 4)
            sinb = bc_mid(rope[:, 1, es, :], 4)
            pv = ps[pi][:, :].rearrange("p (h t d) -> p h t d", h=4, t=2)
            sv = stage[si][:, :].rearrange("p (h t d) -> p h t d", h=4, t=2)
            ta = tmpA[:, :].rearrange("p (h d) -> p h d", h=4)
            tb = tmpB[:, :].rearrange("p (h d) -> p h d", h=4)
            rd = [t_ps[pi], t_rope]
            P.op("dve", lambda e: e.tensor_tensor(out=ta, in0=pv[:, :, 0, :], in1=cosb, op=ALU.mult), reads=rd, writes=[t_tmpA])
            P.op("dve", lambda e: e.tensor_tensor(out=tb, in0=pv[:, :, 1, :], in1=sinb, op=ALU.mult), reads=rd, writes=[t_tmpB])
            P.op("dve", lambda e: e.tensor_tensor(out=sv[:, :, 0, :], in0=ta, in1=tb, op=ALU.subtract),
                 reads=[t_tmpA, t_tmpB], writes=[t_stage[si]])
            P.op("dve", lambda e: e.tensor_tensor(out=ta, in0=pv[:, :, 1, :], in1=cosb, op=ALU.mult), reads=rd, writes=[t_tmpA])
            P.op("dve", lambda e: e.tensor_tensor(out=tb, in0=pv[:, :, 0, :], in1=sinb, op=ALU.mult), reads=rd, writes=[t_tmpB])
            P.op("dve", lambda e: e.tensor_tensor(out=sv[:, :, 1, :], in0=ta, in1=tb, op=ALU.add),
                 reads=[t_tmpA, t_tmpB], writes=[t_stage[si]])
            r0 = tok0 + s * 128
            P.op("sp", lambda e: e.dma_start(out=dst[r0:r0 + 128, col0:col0 + 512], in_=stage[si][:, :]),
                 reads=[t_stage[si]], writes=[d_scr], dma=t_stage[si])
        return f

    def epi_act(dst, tok0, col0, func, bias_blk=None):
        def f(pi, cb, h):
            si = next_stage()
            if bias_blk is None:
                P.op("act", lambda e: e.activation(out=stage[si][:, :], in_=ps[pi][:, :], func=func),
                     reads=[t_ps[pi]], writes=[t_stage[si]])
            else:
                bcol = bias_blk * 4 + cb
                P.op("act", lambda e: e.activation(out=stage[si][:, :], in_=ps[pi][:, :], func=func,
                                                   bias=gbiasT[:, bcol:bcol + 1], scale=1.0),
                     reads=[t_ps[pi], t_gbiasT], writes=[t_stage[si]])
            c = col0 + cb * 128
            t0 = tok0 + h * 512
            P.op("sp", lambda e: e.dma_start(out=dst[c:c + 128, t0:t0 + 512], in_=stage[si][:, :]),
                 reads=[t_stage[si]], writes=[d_scr], dma=t_stage[si])
        return f

    n_u_tiles = cfg.get("n_u_tiles", SEQ // TT)
    n_e_tiles = cfg.get("n_e_tiles", EXT // TT)
    blk = cfg.get("blk_limit", 1000)

    for t in range(n_u_tiles):
        norm_tile(x_b, t * TT, 8)
        for b in range(min(FW // 512, blk)):
            wi = load_w(w_in, OFF_U + b * 512)
            gemm_tm(wi, 8, epi_copy(S_u, t * TT, b * 512, "act" if b % 2 else "dve"))

    for wsrc, wdst, nrow in ((w_ba, WB_ba, AW), (w_bf, WB_bf, FW), (w_out, WB_out, D)):
        for r0 in range(0, nrow, 256):
            P.op("pool", lambda e, wsrc=wsrc, wdst=wdst, r0=r0: e.dma_start(out=wdst[r0:r0 + 256, :], in_=wsrc[r0:r0 + 256, :]),
                 writes=[d_wb], dma=t_wcast)

    for t in range(n_e_tiles):
        norm_tile(x_e, t * TT, 8)
        own = t in (1, 2)
        near = {0: (6, 7), 3: (0, 1)}
        for b in range(min(QKV // 512, blk)):
            subs = 8 if (own or b >= 4) else near[t]
            wi = load_w(w_in, OFF_K + b * 512)
            gemm_tm(wi, subs, epi_rope(S_k, t * TT, b * 512, t * 8))
        for b in range(min(QKV // 512, blk)):
            subs = 8 if (own or b >= 4) else near[t]
            wi = load_w(w_in, OFF_V + b * 512)
            gemm_tm(wi, subs, epi_copy(S_v, t * TT, b * 512, "act"))
        if own:
            o0 = (t - 1) * TT
            for b in range(min(QKV // 512, blk)):
                wi = load_w(w_in, OFF_Q + b * 512)
                gemm_tm(wi, 8, epi_rope(S_q, o0, b * 512, t * 8))
            for b in range(min(AW // 512, blk)):
                wi = load_w(w_in, OFF_ZA + b * 512)
                gemm_fm(wi, 2, epi_act(S_za, o0, b * 512, AF.Silu))
            for b in range(min(FW // 512, blk)):
                wi = load_w(w_in, OFF_ZF + b * 512)
                gemm_fm(wi, 2, epi_act(S_zf, o0, b * 512, AF.Silu))
            for b in range(min(2 * D // 512, blk)):
                wi = load_w(w_in, OFF_G + b * 512)
                gemm_fm(wi, 2, epi_act(S_g, o0, b * 512, AF.Sigmoid, bias_blk=b))


    def next_ps4():
        i = ctr["ps"] % 4
        ctr["ps"] += 1
        return i

    def dma_in(dst_ap, src_ap, t, reads):
        P.op("sp", lambda e: e.dma_start(out=dst_ap, in_=src_ap), reads=reads, writes=[t], dma=t)

    def sub_ap(ap2d, dims):
        a = ap2d.ap
        return bass.AP(ap2d.tensor, ap2d.offset, [list(a[0])] + [list(d) for d in dims])

    if cfg.get("do_p2", True):
        P.barrier()
        sbp["off"] = sb_mark
        acc2 = sb("acc2", [128, 2, OWN], F32); t_acc = T("acc2")
        qktm = [sb("qktm%d" % i, [128, 33, 128], BF16) for i in range(2)]; t_qktm = [T("qktm%d" % i) for i in range(2)]
        vtm = [sb("vtm%d" % i, [128, 17, 128], BF16) for i in range(2)]; t_vtm = [T("vtm%d" % i) for i in range(2)]
        qkT = [sb("qkT%d" % i, [128, 33, 128], BF16) for i in range(2)]; t_qkT = [T("qkT%d" % i) for i in range(2)]
        pT = [sb("pT%d" % i, [128, 256], BF16) for i in range(2)]; t_pT = [T("pT%d" % i) for i in range(2)]
        zat = sb("zat", [128, OWN], BF16); t_zat = T("zat")
        rl = sb("rl", [128, OWN], F32); t_rl = T("rl")
        ato = sb("ato", [128, OWN], BF16); t_ato = T("ato")
        maskAB = sb("maskAB_s", [128, 2, 128], BF16); t_mask = T("mask")
        kval = sb("kval_s", [128, 69], F32); t_kval = T("kval")
        ones = sb("ones", [128, 128], BF16); t_ones = T("ones")
        dma_in(maskAB[:, :, :], mask_d[:, :, :], t_mask, [])
        dma_in(kval[:, :], kval_d[:, :], t_kval, [])
        P.op("pool", lambda e: e.memset(ones[:, :], 1.0), writes=[t_ones])
        SCALE = float(128 ** -0.5)
        it = 0
        for h in range(cfg.get("n_heads", 8)):
            P.op("pool", lambda e: e.memset(acc2[:, :, :], 0.0), writes=[t_acc])
            col = 0
            for g, Dil in enumerate((1, 4, 16)):
                nq = 16 // Dil
                nk = nq + 1
                hc = (g * 8 + h) * 128
                for r in range(Dil):
                    bi = it % 2
                    it += 1
                    q_src = bass.AP(S_q.tensor, r * QKV + hc, [[Dil * QKV, 128], [128 * Dil * QKV, nq], [1, 128]])
                    kb = r + 1024 - 64 * Dil
                    k_src = bass.AP(S_k.tensor, kb * QKV + hc, [[Dil * QKV, 128], [128 * Dil * QKV, nk], [1, 128]])
                    v_src = bass.AP(S_v.tensor, kb * QKV + hc, [[Dil * QKV, 128], [128 * Dil * QKV, nk], [1, 128]])
                    dma_in(qktm[bi][:, 0:nq, :], q_src, t_qktm[bi], [d_scr])
                    dma_in(qktm[bi][:, nq:nq + nk, :], k_src, t_qktm[bi], [d_scr])
                    dma_in(vtm[bi][:, 0:nk, :], v_src, t_vtm[bi], [d_scr])
                    ntr = nq + nk
                    for c0 in range(0, ntr, 8):
                        cn = min(8, ntr - c0)
                        b = ctr["pst"] % 2
                        ctr["pst"] += 1
                        for jj in range(cn):
                            P.op("pe", lambda e, b=b, jj=jj, c0=c0, bi=bi: e.transpose(
                                pst[b][:, jj * 128:(jj + 1) * 128], qktm[bi][:, c0 + jj, :], ident[:, :]),
                                 reads=[t_qktm[bi], t_ident], writes=[t_pst[b]])
                        eng = "act" if (c0 // 8) % 2 else "dve"
                        if eng == "act":
                            P.op("act", lambda e, b=b, c0=c0, cn=cn, bi=bi: e.activation(
                                out=qkT[bi][:, c0:c0 + cn, :], in_=pst[b][:, 0:cn * 128].rearrange("p (a b) -> p a b", b=128),
                                func=AF.Copy), reads=[t_pst[b]], writes=[t_qkT[bi]])
                        else:
                            P.op("dve", lambda e, b=b, c0=c0, cn=cn, bi=bi: e.tensor_copy(
                                out=qkT[bi][:, c0:c0 + cn, :], in_=pst[b][:, 0:cn * 128].rearrange("p (a b) -> p a b", b=128)),
                                 reads=[t_pst[b]], writes=[t_qkT[bi]])
                    def stage1(n, nq=nq, bi=bi, col=col):
                        pS = next_ps()
                        pb = ctr["st"] % 2
                        ctr["st"] += 1
                        for half in range(2):
                            P.op("pe", lambda e, pS=pS, half=half: e.matmul(
                                ps[pS][:, half * 128:(half + 1) * 128], lhsT=qkT[bi][:, nq + n + half, :],
                                rhs=qkT[bi][:, n, :], start=True, stop=False),
                                 reads=[t_qkT[bi]], writes=[t_ps[pS]])
                            P.op("pe", lambda e, pS=pS, half=half: e.matmul(
                                ps[pS][:, half * 128:(half + 1) * 128], lhsT=ident[:, :],
                                rhs=maskAB[:, half, :], start=False, stop=True),
                                 reads=[t_ident, t_mask], writes=[t_ps[pS]])
                        for half in range(2):
                            kc = col + n + half
                            P.op("act", lambda e, pS=pS, half=half, pb=pb, kc=kc: e.activation(
                                out=pT[pb][:, half * 128:(half + 1) * 128], in_=ps[pS][:, half * 128:(half + 1) * 128],
                                func=AF.Exp, bias=kval[:, kc:kc + 1], scale=SCALE),
                                 reads=[t_ps[pS], t_kval], writes=[t_pT[pb]])
                        return pb

                    def stage2(n, pb, bi=bi, r=r, Dil=Dil):
                        pO = next_ps()
                        for half in range(2):
                            P.op("pe", lambda e, pO=pO, half=half: e.matmul(
                                ps[pO][:, 0:128], lhsT=vtm[bi][:, n + half, :], rhs=pT[pb][:, half * 128:(half + 1) * 128],
                                start=(half == 0), stop=(half == 1)),
                                 reads=[t_vtm[bi], t_pT[pb]], writes=[t_ps[pO]])
                        for half in range(2):
                            P.op("pe", lambda e, pO=pO, half=half: e.matmul(
                                ps[pO][:, 128:256], lhsT=ones[:, :], rhs=pT[pb][:, half * 128:(half + 1) * 128],
                                start=(half == 0), stop=(half == 1)),
                                 reads=[t_ones, t_pT[pb]], writes=[t_ps[pO]])
                        st = r + Dil * 128 * n
                        P.op("dve", lambda e, pO=pO, st=st: e.tensor_tensor(
                            out=acc2[:, :, bass.ds(st, 128, step=Dil)], in0=acc2[:, :, bass.ds(st, 128, step=Dil)],
                            in1=ps[pO][:, 0:256].rearrange("p (a b) -> p a b", b=128), op=ALU.add),
                             reads=[t_acc, t_ps[pO]], writes=[t_acc])

                    prev = None
                    for n in range(nq):
                        pb = stage1(n)
                        if prev is not None:
                            stage2(*prev)
                        prev = (n, pb)
                    stage2(*prev)
                    col += nk
            dma_in(zat[:, :], S_za[h * 128:(h + 1) * 128, :], t_zat, [d_scr])
            P.op("dve", lambda e: e.reciprocal(out=rl[:, :], in_=acc2[:, 1, :]), reads=[t_acc], writes=[t_rl])
            P.op("dve", lambda e: e.tensor_tensor(out=acc2[:, 0, :], in0=acc2[:, 0, :], in1=rl[:, :], op=ALU.mult),
                 reads=[t_acc, t_rl], writes=[t_acc])
            P.op("dve", lambda e: e.tensor_tensor(out=ato[:, :], in0=acc2[:, 0, :], in1=zat[:, :], op=ALU.mult),
                 reads=[t_acc, t_zat], writes=[t_ato])
            P.op("sp", lambda e, h=h: e.dma_start(out=S_A[h * 128:(h + 1) * 128, :], in_=ato[:, :]),
                 reads=[t_ato], writes=[d_scr2], dma=t_ato)

    if cfg.get("do_p3", True):
        P.barrier()
        sbp["off"] = sb_mark
        U2 = sb("U2", [128, 64, 256], BF16); t_U2 = T("U2")
        MA = sb("MA_s", [128, 64, 256], BF16); t_MA = T("MA")
        TP = [sb("TP%d" % i, [128, 64 * 256], BF16) for i in range(2)]; t_TP = [T("TP%d" % i) for i in range(2)]
        DC = sb("DC_s", [128, 2, 2, 512], BF16); t_DC = T("DC")
        CS = sb("CS_s", [128, 2, 32], BF16); t_CS = T("CS")
        ZS = [sb("ZS%d" % i, [128, 512], BF16) for i in range(2)]; t_ZS = [T("ZS%d" % i) for i in range(2)]
        zf = [sb("zf%d" % i, [128, OWN], BF16) for i in range(2)]; t_zf = [T("zf%d" % i) for i in range(2)]
        FT = [sb("FT%d" % i, [128, OWN], BF16) for i in range(2)]; t_FT = [T("FT%d" % i) for i in range(2)]
        dma_in(MA[:, :, :], ma_d[:, :, :], t_MA, [])
        dma_in(DC[:, :, :, :], dc_d[:, :, :, :], t_DC, [])
        dma_in(CS[:, :, :], cs_d[:, :, :], t_CS, [])
        ev = 0
        for gi in range(cfg.get("n_fgroups", 8)):
            for dl in range(2):
                u_src = bass.AP(S_u.tensor, dl * FW + gi * 256, [[128 * FW, 64], [2 * FW, 64], [1, 256]])
                dma_in(U2[dl * 64:(dl + 1) * 64, :, :], u_src, t_U2, [d_scr])
            for mp in range(32):
                for cc in range(2):
                    pA = next_ps4()
                    for mm_ in range(2):
                        m = 2 * mp + mm_
                        P.op("pe", lambda e, pA=pA, mm_=mm_, m=m, cc=cc: e.matmul(
                            ps[pA][:, mm_ * 256:(mm_ + 1) * 256], lhsT=U2[:, m, cc * 128:(cc + 1) * 128],
                            rhs=MA[:, m, :], start=True, stop=True),
                             reads=[t_U2, t_MA], writes=[t_ps[pA]])
                    ev += 1
                    if ev % 2:
                        P.op("act", lambda e, pA=pA, mp=mp, cc=cc: e.activation(
                            out=TP[cc][:, mp * 512:(mp + 1) * 512], in_=ps[pA][:, :], func=AF.Copy),
                             reads=[t_ps[pA]], writes=[t_TP[cc]])
                    else:
                        P.op("dve", lambda e, pA=pA, mp=mp, cc=cc: e.tensor_copy(
                            out=TP[cc][:, mp * 512:(mp + 1) * 512], in_=ps[pA][:, :]),
                             reads=[t_ps[pA]], writes=[t_TP[cc]])
            for cc in range(2):
                dma_in(zf[cc][:, :], S_zf[gi * 256 + cc * 128:gi * 256 + (cc + 1) * 128, :], t_zf[cc], [d_scr])
            def stageB(k1):
                nonlocal ev
                pB = next_ps4()
                i = 0
                for cc in range(2):
                    for ri in range(2):
                        P.op("pe", lambda e, pB=pB, cc=cc, ri=ri, i=i: e.matmul(
                            ps[pB][:, :], lhsT=TP[cc][:, bass.ds(ri * 64 + k1, 128, step=128)],
                            rhs=DC[:, cc, ri, :], start=(i == 0), stop=(i == 3)),
                             reads=[t_TP[cc], t_DC], writes=[t_ps[pB]])
                        i += 1
                zb = k1 % 2
                ev += 1
                if ev % 2:
                    P.op("act", lambda e, pB=pB, zb=zb: e.activation(out=ZS[zb][:, :], in_=ps[pB][:, :], func=AF.Copy),
                         reads=[t_ps[pB]], writes=[t_ZS[zb]])
                else:
                    P.op("dve", lambda e, pB=pB, zb=zb: e.tensor_copy(out=ZS[zb][:, :], in_=ps[pB][:, :]),
                         reads=[t_ps[pB]], writes=[t_ZS[zb]])

            def stageC(k1):
                zb = k1 % 2
                k1l = k1 % 16
                for cc2 in range(2):
                    for ri2 in range(2):
                        P.op("pe", lambda e, cc2=cc2, ri2=ri2: e.matmul(
                            ps[4 + cc2][:, k1l * 32:(k1l + 1) * 32],
                            lhsT=ZS[zb][:, ri2 * 256 + cc2 * 128:ri2 * 256 + (cc2 + 1) * 128],
                            rhs=CS[:, ri2, :], start=(ri2 == 0), stop=(ri2 == 1)),
                             reads=[t_ZS[zb], t_CS], writes=[t_ps[4 + cc2]])
                if k1l == 15:
                    k0 = (k1 // 16) * 16
                    for cc2 in range(2):
                        dst = sub_ap(FT[cc2][:, k0:k0 + 1], [[1, 16], [64, 32]])
                        zsrc = sub_ap(zf[cc2][:, k0:k0 + 1], [[1, 16], [64, 32]])
                        P.op("dve", lambda e, cc2=cc2, dst=dst, zsrc=zsrc: e.tensor_tensor(
                            out=dst, in0=ps[4 + cc2][:, :].rearrange("p (a b) -> p a b", b=32), in1=zsrc, op=ALU.mult),
                             reads=[t_ps[4 + cc2], t_zf[cc2]], writes=[t_FT[cc2]])

            for k1 in range(64):
                stageB(k1)
                if k1 > 0:
                    stageC(k1 - 1)
            stageC(63)
            for cc2 in range(2):
                r0 = gi * 256 + cc2 * 128
                P.op("sp", lambda e, cc2=cc2, r0=r0: e.dma_start(out=S_F[r0:r0 + 128, :], in_=FT[cc2][:, :]),
                     reads=[t_FT[cc2]], writes=[d_scr2], dma=t_FT[cc2])

    d_out = T("d_out")
    if cfg.get("do_p4", True):
        P.barrier()
        sbp["off"] = sb_mark
        T4 = 512
        wsl = [sb("wsl%d" % i, [128, KC, 256], BF16) for i in range(2)]; t_wsl = [T("wsl%d" % i) for i in range(2)]
        mixT = sb("mixT", [128, KC, T4], BF16); t_mix = T("mixT")
        ATt = sb("ATt", [128, 8, T4], BF16); t_ATt = T("ATt")
        FTt = sb("FTt", [128, 16, T4], BF16); t_FTt = T("FTt")
        sg = [sb("sg%d" % i, [128, 2, T4], BF16) for i in range(2)]; t_sg = [T("sg%d" % i) for i in range(2)]
        tm1 = sb("tm1", [128, T4], F32); t_tm1 = T("tm1")
        tm2 = sb("tm2", [128, T4], F32); t_tm2 = T("tm2")
        xo = [sb("xo%d" % i, [128, D], F32) for i in range(4)]; t_xo = [T("xo%d" % i) for i in range(4)]
        fg = sb("fg", [128, D], F32); t_fg = T("fg")
        ss4 = sb("ss4", [128, 8], F32); t_ss4 = T("ss4")
        dma_in(fg[:, :], fgain_d[0:1, :].partition_broadcast(128), t_fg, [])
        wc = 0
        for tt in range(cfg.get("n_p4_tiles", OWN // T4)):
            tok0 = tt * T4
            dma_in(ATt[:, :, :], S_A[:, tok0:tok0 + T4].rearrange("(k p) t -> p k t", p=128), t_ATt, [d_scr2])
            dma_in(FTt[:, :, :], S_F[:, tok0:tok0 + T4].rearrange("(k p) t -> p k t", p=128), t_FTt, [d_scr2])
            for s4 in range(4):
                r0 = tok0 + s4 * 128
                dma_in(xo[s4][:, :], x_e[HALO + r0:HALO + r0 + 128, :], t_xo[s4], [])
            for ob in range(D // 256):
                wa = wc % 2; wc += 1
                a_src = WB_ba[:, ob * 256:(ob + 1) * 256].rearrange("(k p) c -> p k c", p=128)
                P.op("sp", lambda e, wa=wa, a_src=a_src: e.dma_start(out=wsl[wa][:, 0:8, :], in_=a_src),
                     reads=[d_wb], writes=[t_wsl[wa]], dma=t_wsl[wa])
                wf = wc % 2; wc += 1
                f_src = WB_bf[:, ob * 256:(ob + 1) * 256].rearrange("(k p) c -> p k c", p=128)
                P.op("sp", lambda e, wf=wf, f_src=f_src: e.dma_start(out=wsl[wf][:, 0:16, :], in_=f_src),
                     reads=[d_wb], writes=[t_wsl[wf]], dma=t_wsl[wf])
                for cb in range(2):
                    oc = ob * 2 + cb
                    si = oc % 2
                    g_src = bass.AP(S_g.tensor, oc * 128 * OWN + tok0, [[OWN, 128], [D * OWN, 2], [1, T4]])
                    dma_in(sg[si][:, :, :], g_src, t_sg[si], [d_scr])
                    p1 = next_ps()
                    for k in range(8):
                        P.op("pe", lambda e, p1=p1, k=k, cb=cb, wa=wa: e.matmul(
                            ps[p1][:, :], lhsT=wsl[wa][:, k, cb * 128:(cb + 1) * 128], rhs=ATt[:, k, :],
                            start=(k == 0), stop=(k == 7)), reads=[t_wsl[wa], t_ATt], writes=[t_ps[p1]])
                    p2 = next_ps()
                    for k in range(16):
                        P.op("pe", lambda e, p2=p2, k=k, cb=cb, wf=wf: e.matmul(
                            ps[p2][:, :], lhsT=wsl[wf][:, k, cb * 128:(cb + 1) * 128], rhs=FTt[:, k, :],
                            start=(k == 0), stop=(k == 15)), reads=[t_wsl[wf], t_FTt], writes=[t_ps[p2]])
                    P.op("dve", lambda e, p1=p1, si=si: e.tensor_tensor(out=tm1[:, :], in0=ps[p1][:, :], in1=sg[si][:, 0, :],
                                                                      op=ALU.mult), reads=[t_ps[p1], t_sg[si]], writes=[t_tm1])
                    P.op("dve", lambda e, p2=p2, si=si: e.tensor_tensor(out=tm2[:, :], in0=ps[p2][:, :], in1=sg[si][:, 1, :],
                                                                      op=ALU.mult), reads=[t_ps[p2], t_sg[si]], writes=[t_tm2])
                    P.op("pool", lambda e, oc=oc: e.tensor_tensor(out=mixT[:, oc, :], in0=tm1[:, :], in1=tm2[:, :], op=ALU.add),
                         reads=[t_tm1, t_tm2], writes=[t_mix])
            for cb in range(D // 256):
                wo = wc % 2; wc += 1
                o_src = WB_out[:, cb * 256:(cb + 1) * 256].rearrange("(k p) c -> p k c", p=128)
                P.op("sp", lambda e, wo=wo, o_src=o_src: e.dma_start(out=wsl[wo][:, :, :], in_=o_src),
                     reads=[d_wb], writes=[t_wsl[wo]], dma=t_wsl[wo])
                for s4 in range(4):
                    po = next_ps()
                    for k in range(KC):
                        P.op("pe", lambda e, po=po, k=k, s4=s4, wo=wo: e.matmul(
                            ps[po][:, 0:256], lhsT=mixT[:, k, s4 * 128:(s4 + 1) * 128], rhs=wsl[wo][:, k, :],
                            start=(k == 0), stop=(k == KC - 1)), reads=[t_mix, t_wsl[wo]], writes=[t_ps[po]])
                    P.op("dve", lambda e, po=po, s4=s4, cb=cb: e.tensor_tensor(
                        out=xo[s4][:, cb * 256:(cb + 1) * 256], in0=xo[s4][:, cb * 256:(cb + 1) * 256], in1=ps[po][:, 0:256],
                        op=ALU.add), reads=[t_ps[po], t_xo[s4]], writes=[t_xo[s4]])
            for s4 in range(4):
                P.op("act", lambda e, s4=s4: e.activation(out=mixT[:, 0:8, :].rearrange("p a b -> p (a b)"), in_=xo[s4][:, :],
                                                          func=AF.Square, accum_out=ss4[:, 0:1]),
                     reads=[t_xo[s4]], writes=[t_mix, t_ss4])
                P.op("act", lambda e: e.activation(out=ss4[:, 1:2], in_=ss4[:, 0:1], func=AF.Sqrt, bias=epsb[:, 0:1],
                                                   scale=1.0 / D), reads=[t_ss4, t_ident], writes=[t_ss4])
                P.op("dve", lambda e: e.reciprocal(out=ss4[:, 2:3], in_=ss4[:, 1:2]), reads=[t_ss4], writes=[t_ss4])
                P.op("dve", lambda e, s4=s4: e.scalar_tensor_tensor(out=xo[s4][:, :], in0=xo[s4][:, :], scalar=ss4[:, 2:3],
                                                                    in1=fg[:, :], op0=ALU.mult, op1=ALU.mult),
                     reads=[t_xo[s4], t_ss4, t_fg], writes=[t_xo[s4]])
                r0 = tok0 + s4 * 128
                P.op("sp", lambda e, s4=s4, r0=r0: e.dma_start(out=out_d[r0:r0 + 128, :], in_=xo[s4][:, :]),
                     reads=[t_xo[s4]], writes=[d_out], dma=t_xo[s4])
    else:
        P.op("pool", lambda e: e.memset(xbuf[0][:, :], 0.0), reads=[d_scr], writes=[t_xbuf[0]])
        for r in range(OWN // 128):
            P.op("sp", lambda e, r=r: e.dma_start(out=out_d[r * 128:(r + 1) * 128, :], in_=xbuf[0][:, :]),
                 reads=[t_xbuf[0]], writes=[d_out], dma=t_xbuf[0])
    P.op("sp", None, reads=[d_out, d_scr, d_scr2])

    P.emit()
    return nc


def host_inputs(inputs, cfg):
    x = np.asarray(inputs["x"], dtype=np.float32)
    w_in = np.ascontiguousarray(np.asarray(inputs["w_in"], dtype=np.float32)[0])
    w_ba = np.ascontiguousarray(np.asarray(inputs["w_branch_attn"], dtype=np.float32)[0])
    w_bf = np.ascontiguousarray(np.asarray(inputs["w_branch_fourier"], dtype=np.float32)[0])
    w_out = np.ascontiguousarray(np.asarray(inputs["w_out"], dtype=np.float32)[0])
    gainT = np.ascontiguousarray(np.asarray(inputs["norm_gain"], dtype=np.float32)[0].reshape(KC, 128).T)
    gbiasT = np.ascontiguousarray(np.asarray(inputs["gate_bias"], dtype=np.float32)[0].reshape(64, 128).T)
    half = 64
    inv_freq = (10000.0 ** (-np.arange(half, dtype=np.float32) * np.float32(2.0 / 128))).astype(np.float32)
    bf = ml_dtypes.bfloat16
    fgain = np.asarray(inputs["final_norm_gain"], dtype=np.float32).reshape(1, D)
    kj = np.arange(128)[:, None]
    qi = np.arange(128)[None, :]
    maskAB = np.stack([np.where(kj >= qi, 0.0, -30000.0), np.where(kj <= qi, 0.0, -30000.0)], 1).astype(bf)
    n1 = np.arange(64)[None, :, None, None, None, None]
    dl = np.arange(2)[:, None, None, None, None, None]
    m_ = np.arange(64)[None, None, :, None, None, None]
    dl2 = np.arange(2)[None, None, None, :, None, None]
    k1_ = np.arange(64)[None, None, None, None, None, :]
    th = 2.0 * np.pi * (((128 * n1 + 2 * m_ + dl) * k1_) % SEQ) / SEQ
    same = (dl == dl2).astype(np.float64)
    MAc = np.broadcast_to(same * np.cos(th), (2, 64, 64, 2, 1, 64))
    MAs = np.broadcast_to(-same * np.sin(th), (2, 64, 64, 2, 1, 64))
    MA = (np.concatenate([MAc, MAs], 4) / np.sqrt(SEQ)).reshape(128, 64, 256).astype(bf)
    cch = (np.arange(2)[None, :, None] * 128 + np.arange(128)[:, None, None])
    cp = np.arange(256)[None, None, :]
    ph = 2.0 * np.pi * ((cch * cp) % 256) / 256
    DC = np.zeros((128, 2, 2, 512), np.float64)
    DC[:, :, 0, 0:256] = np.cos(ph); DC[:, :, 0, 256:512] = -np.sin(ph)
    DC[:, :, 1, 0:256] = np.sin(ph); DC[:, :, 1, 256:512] = np.cos(ph)
    DC = (DC / 16.0).astype(bf)
    maps = []
    for c in range(8):
        b, j = c // 4, c % 4
        e0 = j * OWN - HALO
        x_e = np.zeros((EXT, D), np.float32)
        lo, hi = max(e0, 0), min(e0 + EXT, SEQ)
        x_e[lo - e0:hi - e0] = x[b, lo:hi]
        pos = (np.arange(EXT) + e0).astype(np.float32)
        ang = pos[:, None] * inv_freq[None, :]
        rope = np.stack([np.cos(ang), np.sin(ang)], 0).astype(np.float32)
        rope = np.ascontiguousarray(rope.reshape(2, EXT // 128, 128, 64).transpose(2, 0, 1, 3))
        kcols = []
        pp = np.arange(128)
        for Dil in (1, 4, 16):
            nk = 16 // Dil + 1
            for r in range(Dil):
                for m in range(nk):
                    e = r + 1024 - 64 * Dil + Dil * (128 * m + pp)
                    gpos = e0 + e
                    kcols.append(np.where((gpos >= 0) & (gpos < SEQ), 0.0, -30000.0))
        kval = np.ascontiguousarray(np.stack(kcols, 1).astype(np.float32))
        assert kval.shape == (128, 69)
        n2 = np.arange(128)[:, None]
        k2 = (32 * j + np.arange(32))[None, :]
        ps_ = 2.0 * np.pi * ((n2 * k2) % 128) / 128
        CS = np.ascontiguousarray(np.stack([np.cos(ps_), np.sin(ps_)], 1).astype(bf))
        maps.append({"x_b": np.ascontiguousarray(x[b]), "x_e": x_e, "w_in": w_in, "w_ba": w_ba, "w_bf": w_bf,
                     "w_out": w_out, "gainT": gainT, "gbiasT": gbiasT, "rope": rope, "fgain": fgain,
                     "maskAB": np.ascontiguousarray(maskAB), "kval": kval, "MA": np.ascontiguousarray(MA),
                     "DC": np.ascontiguousarray(DC), "CS": CS})
    return maps


def kernel(**inputs):
    cfg = {}
    nc = build(cfg)
    maps = host_inputs(inputs, cfg)
    res = run_bass_kernel_spmd(nc, maps, core_ids=list(range(8)))
    out = np.zeros((2, SEQ, D), np.float32)
    for c in range(8):
        b, j = c // 4, c % 4
        out[b, j * OWN:(j + 1) * OWN] = res.results[c]["out"]
    return out
```
